# Optimizing a Trainium2 kernel written in Bass

```python
import jax, jax.numpy as jnp
from jax import lax
import numpy as np

D_MODEL = 2048
BATCH = 8
SEQ = 2048
DEPTH = 1

D_MIX = D_MODEL
D_CONV = D_MIX // 2
CONV_WIDTH = 31
N_HEADS = 8
QK_NOPE_DIM = 128
QK_ROPE_DIM = 64
V_HEAD_DIM = 128
Q_LORA_RANK = 768
KV_LORA_RANK = 512
D_ATTN = N_HEADS * V_HEAD_DIM
D_IN = 2 * D_CONV + Q_LORA_RANK + KV_LORA_RANK + QK_ROPE_DIM
D_FF = 5632
FFN_RES_WEIGHT = 0.5
N_SUBLAYERS = 3
ROPE_BASE = 10000.0
Q_BLOCK = 128
EPS = 1e-6
POS_OFFSET_MAX = 4096
ADA_SCALE = 0.5

kernel_name = 'hymba_conformer_mla_macaron_adaln'


def rms_norm(x, g):
    xf = x.astype(jnp.float32)
    y = xf * lax.rsqrt(jnp.mean(xf * xf, axis=-1, keepdims=True) + EPS)
    return (y * g.astype(jnp.float32)).astype(x.dtype)


def layer_norm(x, g, b):
    xf = x.astype(jnp.float32)
    mu = jnp.mean(xf, axis=-1, keepdims=True)
    xc = xf - mu
    var = jnp.mean(xc * xc, axis=-1, keepdims=True)
    y = xc * lax.rsqrt(var + EPS) * g.astype(jnp.float32) + b.astype(jnp.float32)
    return y.astype(x.dtype)


def modulate(u, shift, scale):
    return u * (1.0 + scale[:, None, :]) + shift[:, None, :]


def swiglu(u, w_gate, w_up, w_down):
    return (jax.nn.silu(u @ w_gate) * (u @ w_up)) @ w_down


def apply_rope(x, cos, sin):
    half = x.shape[-1] // 2
    x1, x2 = x[..., :half], x[..., half:]
    cos = cos.astype(x.dtype)
    sin = sin.astype(x.dtype)
    return jnp.concatenate([x1 * cos - x2 * sin, x2 * cos + x1 * sin], axis=-1)


def conformer_conv(a, w_dw, b_dw, ln_g, ln_b):
    val, gate = jnp.split(a, 2, axis=-1)
    h = val * jax.nn.sigmoid(gate)
    h = lax.conv_general_dilated(
        h, w_dw[:, None, :].astype(h.dtype), window_strides=(1,),
        padding=[(CONV_WIDTH - 1, 0)],
        dimension_numbers=('NWC', 'WIO', 'NWC'),
        feature_group_count=D_CONV) + b_dw
    return jax.nn.silu(layer_norm(h, ln_g, ln_b))


def mla_causal_attention(q_nope, q_rope, k_nope, k_rope, v):
    S = q_nope.shape[1]
    scale = (QK_NOPE_DIM + QK_ROPE_DIM) ** -0.5
    outs = []
    for i in range(S // Q_BLOCK):
        q0 = i * Q_BLOCK
        kend = q0 + Q_BLOCK
        s = (jnp.einsum('bqhd,bkhd->bhqk', q_nope[:, q0:kend], k_nope[:, :kend])
             + jnp.einsum('bqhr,bkr->bhqk', q_rope[:, q0:kend], k_rope[:, :kend]))
        s = s.astype(jnp.float32) * scale
        qpos = q0 + jnp.arange(Q_BLOCK)
        mask = qpos[:, None] >= jnp.arange(kend)[None, :]
        s = jnp.where(mask[None, None], s, -jnp.inf)
        p = jax.nn.softmax(s, axis=-1).astype(v.dtype)
        outs.append(jnp.einsum('bhqk,bkhd->bqhd', p, v[:, :kend]))
    return jnp.concatenate(outs, axis=1)


def token_mix(u, cos, sin, w_in, w_dw, b_dw, ln_conv_g, ln_conv_b, g_q_lat, w_uq,
              g_kv_lat, w_uk, w_uv, g_conv_out, g_attn_out, w_out):
    B, S, _ = u.shape
    z = u @ w_in
    i1 = 2 * D_CONV
    i2 = i1 + Q_LORA_RANK
    i3 = i2 + KV_LORA_RANK
    conv_in = z[..., :i1]
    q_lat = z[..., i1:i2]
    kv_lat = z[..., i2:i3]
    k_rope_raw = z[..., i3:]
    conv_out = conformer_conv(conv_in, w_dw, b_dw, ln_conv_g, ln_conv_b)
    q = (rms_norm(q_lat, g_q_lat) @ w_uq).reshape(B, S, N_HEADS, QK_NOPE_DIM + QK_ROPE_DIM)
    q_nope = q[..., :QK_NOPE_DIM]
    q_rope = apply_rope(q[..., QK_NOPE_DIM:], cos, sin)
    c_kv = rms_norm(kv_lat, g_kv_lat)
    k_nope = (c_kv @ w_uk).reshape(B, S, N_HEADS, QK_NOPE_DIM)
    v = (c_kv @ w_uv).reshape(B, S, N_HEADS, V_HEAD_DIM)
    k_rope = apply_rope(k_rope_raw[:, :, None, :], cos, sin)[:, :, 0, :]
    attn = mla_causal_attention(q_nope, q_rope, k_nope, k_rope, v).reshape(B, S, D_ATTN)
    merged = jnp.concatenate([rms_norm(conv_out, g_conv_out), rms_norm(attn, g_attn_out)], axis=-1)
    return merged @ w_out


def setup_inputs(seed: int = 0) -> dict:
    key = jax.random.key(seed)
    ks = iter(jax.random.split(key, 40))
    L, D = DEPTH, D_MODEL
    f32 = jnp.float32

    def dense(shape, fan_in, mult=1.0):
        return jax.random.normal(next(ks), shape, f32) * (mult * fan_in ** -0.5)

    def gain(n):
        return 1.0 + 0.05 * jax.random.normal(next(ks), (L, n), f32)

    def bias(n):
        return 0.02 * jax.random.normal(next(ks), (L, n), f32)

    x = jax.random.normal(next(ks), (BATCH, SEQ, D), f32)
    c = jax.random.normal(next(ks), (BATCH, D), f32)
    offset = jax.random.randint(next(ks), (BATCH, 1), 0, POS_OFFSET_MAX, dtype=jnp.int32)
    positions = offset + jnp.arange(SEQ, dtype=jnp.int32)[None, :]
    return {
        'x': x,
        'c': c,
        'positions': positions,
        'w_ada': dense((L, D, N_SUBLAYERS * 3 * D), D, ADA_SCALE),
        'b_ada': bias(N_SUBLAYERS * 3 * D),
        'g_pre_ffn1': gain(D),
        'w1_gate': dense((L, D, D_FF), D),
        'w1_up': dense((L, D, D_FF), D),
        'w1_down': dense((L, D_FF, D), D_FF),
        'g_post_ffn1': gain(D),
        'g_pre_mix': gain(D),
        'w_in': dense((L, D, D_IN), D),
        'w_dw': dense((L, CONV_WIDTH, D_CONV), CONV_WIDTH),
        'b_dw': bias(D_CONV),
        'ln_conv_g': gain(D_CONV),
        'ln_conv_b': bias(D_CONV),
        'g_q_lat': gain(Q_LORA_RANK),
        'w_uq': dense((L, Q_LORA_RANK, N_HEADS * (QK_NOPE_DIM + QK_ROPE_DIM)), Q_LORA_RANK),
        'g_kv_lat': gain(KV_LORA_RANK),
        'w_uk': dense((L, KV_LORA_RANK, N_HEADS * QK_NOPE_DIM), KV_LORA_RANK),
        'w_uv': dense((L, KV_LORA_RANK, N_HEADS * V_HEAD_DIM), KV_LORA_RANK),
        'g_conv_out': gain(D_CONV),
        'g_attn_out': gain(D_ATTN),
        'w_out': dense((L, D_MIX, D), D_MIX),
        'g_post_mix': gain(D),
        'g_pre_ffn2': gain(D),
        'w2_gate': dense((L, D, D_FF), D),
        'w2_up': dense((L, D, D_FF), D),
        'w2_down': dense((L, D_FF, D), D_FF),
        'g_post_ffn2': gain(D),
    }


def reference(x, c, positions, w_ada, b_ada, g_pre_ffn1, w1_gate, w1_up, w1_down, g_post_ffn1,
              g_pre_mix, w_in, w_dw, b_dw, ln_conv_g, ln_conv_b, g_q_lat, w_uq, g_kv_lat, w_uk, w_uv,
              g_conv_out, g_attn_out, w_out, g_post_mix, g_pre_ffn2, w2_gate, w2_up, w2_down, g_post_ffn2):
    B, S, D = x.shape
    half = QK_ROPE_DIM // 2
    inv_freq = ROPE_BASE ** (-jnp.arange(half, dtype=jnp.float32) / half)
    ang = positions.astype(jnp.float32)[:, :, None, None] * inv_freq
    cos, sin = jnp.cos(ang), jnp.sin(ang)
    sc = jax.nn.silu(c)
    for l in range(DEPTH):
        mod = (sc @ w_ada[l] + b_ada[l]).reshape(B, N_SUBLAYERS, 3, D)
        u = modulate(rms_norm(x, g_pre_ffn1[l]), mod[:, 0, 0], mod[:, 0, 1])
        y = swiglu(u, w1_gate[l], w1_up[l], w1_down[l])
        x = x + FFN_RES_WEIGHT * mod[:, 0, 2][:, None, :] * rms_norm(y, g_post_ffn1[l])
        u = modulate(rms_norm(x, g_pre_mix[l]), mod[:, 1, 0], mod[:, 1, 1])
        y = token_mix(u, cos, sin, w_in[l], w_dw[l], b_dw[l], ln_conv_g[l], ln_conv_b[l],
                      g_q_lat[l], w_uq[l], g_kv_lat[l], w_uk[l], w_uv[l],
                      g_conv_out[l], g_attn_out[l], w_out[l])
        x = x + mod[:, 1, 2][:, None, :] * rms_norm(y, g_post_mix[l])
        u = modulate(rms_norm(x, g_pre_ffn2[l]), mod[:, 2, 0], mod[:, 2, 1])
        y = swiglu(u, w2_gate[l], w2_up[l], w2_down[l])
        x = x + FFN_RES_WEIGHT * mod[:, 2, 2][:, None, :] * rms_norm(y, g_post_ffn2[l])
    return x
```

```python
import math
import numpy as np
import concourse.bass as bass
import concourse.mybir as mybir
from concourse.bass_utils import run_bass_kernel_spmd

F32 = mybir.dt.float32
BF16 = mybir.dt.bfloat16
I32 = mybir.dt.int32
AF = mybir.ActivationFunctionType
ALU = mybir.AluOpType

D = 2048
S = 2048
T = 512
NT = S // T
DFF = 5632
NFC = DFF // 128
NDC = D // 128
EPS = 1e-6
NHEAD = 8
SCALE = (128 + 64) ** -0.5
NSLOT = 6
GRAN = 256

PV = {}
_off = 0
for _n, _w in [("g_pre_ffn1", 16), ("g_post_ffn1", 16), ("g_pre_mix", 16), ("g_post_mix", 16),
               ("g_pre_ffn2", 16), ("g_post_ffn2", 16), ("b_ada", 144), ("b_dw", 8), ("ln_conv_g", 8),
               ("ln_conv_b", 8), ("g_conv_out", 8), ("g_attn_out", 8), ("g_q_lat", 6), ("g_kv_lat", 4),
               ("w_dw", 8 * 31), ("c", 16)]:
    PV[_n] = (_off, _w)
    _off += _w
NPV = _off
C_ID, C_MASK, C_INVF, C_SIGN, NCONST = 0, 128, 128 + 896, 128 + 896 + 1, 128 + 896 + 2


class V:
    def __init__(self, buf, off, dtype, n, p0=0, p1=128, shape=None):
        self.buf, self.off, self.dtype, self.n, self.p0, self.p1, self.shape = buf, off, dtype, n, p0, p1, shape
        self.sz = 4 if dtype in (F32, I32) else 2

    @property
    def ap(self):
        t = self.buf.t if self.dtype == self.buf.dtype else self.buf.t.bitcast(self.dtype)
        e0 = self.off // self.sz
        a = t[self.p0:self.p1, e0:e0 + self.n]
        if self.shape is not None:
            a = a.rearrange("p (a b) -> p a b", a=self.shape[0])
        return a

    @property
    def k(self):
        if self.buf.name.startswith("ps"):
            return [(self.buf.name, 0)]
        g0 = self.off // GRAN
        g1 = (self.off + self.n * self.sz - 1) // GRAN
        return [(self.buf.name, g) for g in range(g0, g1 + 1)]

    def sub(self, a, b):
        return V(self.buf, self.off + a * self.sz, self.dtype, b - a, self.p0, self.p1)

    def part(self, p0, p1):
        return V(self.buf, self.off, self.dtype, self.n, p0, p1)

    def as3(self, a):
        return V(self.buf, self.off, self.dtype, self.n, self.p0, self.p1, shape=(a, self.n // a))


class Buf:
    def __init__(self, name, t, dtype):
        self.name, self.t, self.dtype = name, t, dtype


class Prog:
    def __init__(self):
        self.ops = []
        self.res = {}
        self.dry = False

    def op(self, eng, fn, R=(), W=(), dsem=None):
        if self.dry:
            return None
        idx = len(self.ops)
        deps = set()
        for r in R:
            st = self.res.get(r)
            if st is None:
                st = self.res[r] = [None, []]
            if st[0] is not None:
                deps.add(st[0])
        for w in W:
            st = self.res.get(w)
            if st is None:
                st = self.res[w] = [None, []]
            if st[0] is not None:
                deps.add(st[0])
            deps.update(st[1])
        for r in R:
            self.res[r][1].append(idx)
        for w in W:
            st = self.res[w]
            st[0] = idx
            st[1] = []
        deps.discard(idx)
        self.ops.append(dict(eng=eng, fn=fn, deps=deps, dsem=dsem, need=False, sig=None))
        return idx

    def finalize(self):
        ops = self.ops
        for o in ops:
            for d in o["deps"]:
                if ops[d]["eng"] == "pe" and o["eng"] == "pe":
                    continue
                ops[d]["need"] = True
        cnt = {}
        for o in ops:
            if o["dsem"] is not None:
                key = ("dma", o["dsem"])
                cnt[key] = cnt.get(key, 0) + 16
                o["sig"] = (key, cnt[key])
            elif o["need"]:
                key = ("eng", o["eng"])
                cnt[key] = cnt.get(key, 0) + 1
                o["sig"] = (key, cnt[key])
        return sorted(cnt.keys())

    def emit(self, eng_name, e, sems):
        ops = self.ops
        waited = {}
        for o in ops:
            if o["eng"] != eng_name:
                continue
            need = {}
            for d in o["deps"]:
                od = ops[d]
                if od["eng"] == "pe" and eng_name == "pe":
                    continue
                key, val = od["sig"]
                if need.get(key, 0) < val:
                    need[key] = val
            for key, val in need.items():
                if waited.get(key, 0) < val:
                    e.wait_ge(sems[key], val)
                    waited[key] = val
            if o["fn"] is None:
                continue
            ins = o["fn"](e)
            if o["sig"] is not None:
                key, _ = o["sig"]
                ins.then_inc(sems[key], 16 if key[0] == "dma" else 1)


class WStream:
    def __init__(self, P, slots):
        self.P, self.slots = P, slots
        self.seq = []
        self.i = 0
        self.issued = 0
        self.look = 1

    def reset(self):
        self.i = 0
        self.issued = 0

    def _issue(self, j):
        src, ncols = self.seq[j]
        sl = self.slots[j % NSLOT].sub(0, ncols)
        self.P.op("pool", lambda e, s=src, d=sl: e.dma_start(out=d.ap, in_=s), R=[], W=sl.k,
                  dsem="slot%d" % (j % NSLOT))

    def next(self, src, ncols):
        if self.P.dry:
            self.seq.append((src, ncols))
            j = self.i
            self.i += 1
            return self.slots[j % NSLOT].sub(0, ncols)
        j = self.i
        while self.issued < min(len(self.seq), j + NSLOT - self.look):
            self._issue(self.issued)
            self.issued += 1
        self.i += 1
        return self.slots[j % NSLOT].sub(0, ncols)


def build_program():
    nc = bass.Bass("TRN2", target_bir_lowering=False)
    dr = {}

    def din(name, shape, dt=F32):
        dr[name] = nc.dram_tensor(name, shape, dt, kind="ExternalInput").ap()
        return dr[name]

    xT = din("xT", [D, S])
    posb = din("posb", [128, S], I32)
    pvec_d = din("pvec", [128, NPV])
    const_d = din("consts", [128, NCONST])
    wg = [din("wg1", [NFC, 128, 2048]), din("wg2", [NFC, 128, 2048])]
    wu = [din("wu1", [NFC, 128, 2048]), din("wu2", [NFC, 128, 2048])]
    wd = [din("wd1", [4, 11, 128, 2048]), din("wd2", [4, 11, 128, 2048])]
    win = din("win", [26, 128, 2048])
    winr = din("winr", [2, 128, 2048])
    wuqn = din("wuqn", [8, 128, 768])
    wuqr = din("wuqr", [4, 2, 128, 768])
    wuk = din("wuk", [8, 128, 512])
    wuv = din("wuv", [2, 128, 2048])
    wout = din("wout", [16, 128, 2048])
    wada = din("wada", [144, 128, 2048])
    outT = nc.dram_tensor("outT", [D, S], F32, kind="ExternalOutput").ap()

    P = Prog()
    import contextlib
    with contextlib.ExitStack() as es:
        def sb(name, nbytes, dt=F32):
            t = es.enter_context(nc.sbuf_tensor("sb_" + name, [128, nbytes // (4 if dt in (F32, I32) else 2)], dt))
            return Buf(name, t, dt)

        b_x = sb("xres", NDC * T * 4)
        b_uy = sb("uy", NDC * T * 2, BF16)
        b_h = sb("h", NFC * T * 2, BF16)
        b_kn = sb("knope", NHEAD * S * 2, BF16)
        b_v = sb("vc", 16 * 1024 * 2, BF16)
        b_kr = sb("krope", S * 2, BF16)
        b_sl = sb("slots", NSLOT * 4096, BF16)
        b_ring = sb("ring", 4 * 1024, BF16)
        b_tf = sb("tmpf", 2 * 2048)
        b_rs = sb("rstd", 2 * 2048)
        b_cs = sb("cossin", 2 * 2048)
        b_pv = sb("pvec", NPV * 4)
        b_md = sb("modsb", (144 + 9 * 16) * 4)
        b_cb = sb("constb", (128 + 896 + 128) * 2, BF16)
        b_cf = sb("constf", 130 * 4)
        b_halo = sb("halo", 8 * 32 * 2, BF16)
        b_scb = sb("scb", 16 * 2, BF16)
        b_bs = sb("bsb", 2 * NFC * 2 * 4)
        b_shb = sb("shb", 3 * 16 * 2, BF16)
        b_bsm = sb("bsm", 28 * 4)
        pst = []
        for b in range(8):
            t = es.enter_context(nc.psum_tensor("ps%d" % b, [128, 512], F32))
            pst.append(Buf("ps%d" % b, t, F32))

        def PS(b, n=512, c0=0, p0=0, p1=128):
            return V(pst[b], c0 * 4, F32, n, p0, p1)

        xres = [V(b_x, c * T * 4, F32, T) for c in range(NDC)]
        xres_all = V(b_x, 0, F32, NDC * T)
        u = [V(b_uy, c * T * 2, BF16, T) for c in range(NDC)]
        hh = [V(b_h, c * T * 2, BF16, T) for c in range(NFC)]
        knope = [V(b_kn, hd * S * 2, BF16, S) for hd in range(NHEAD)]
        vc = [V(b_v, kb * 1024 * 2, BF16, 1024) for kb in range(16)]
        krope = V(b_kr, 0, BF16, S)
        slots = [V(b_sl, i * 4096, BF16, 2048) for i in range(NSLOT)]
        ring = [V(b_ring, i * 1024, BF16, T) for i in range(4)]
        tmpf = [V(b_tf, i * 2048, F32, T) for i in range(2)]
        rstd = [V(b_rs, i * 2048, F32, T) for i in range(2)]
        rstd.append(rstd[0])
        cosf = V(b_cs, 0, F32, T)
        sinf = V(b_cs, 2048, F32, T)

        def pv(name, c0=0, n=None):
            o, w = PV[name]
            n = w - c0 if n is None else n
            return V(b_pv, (o + c0) * 4, F32, n)

        modsb = V(b_md, 0, F32, 144)
        gs = [V(b_md, (144 + s * 16) * 4, F32, 16) for s in range(3)]
        gg = [V(b_md, (144 + 48 + s * 16) * 4, F32, 16) for s in range(3)]
        def shiftv(s, c):
            return modsb.sub((3 * s) * 16 + c, (3 * s) * 16 + c + 1)
        ident_bf = V(b_cb, 0, BF16, 128)
        masks_bf = [V(b_cb, (128 + 384 - 128 * j) * 2, BF16, 512) for j in range(4)]
        ones_bf = V(b_cb, (128 + 896) * 2, BF16, 128)
        ident_f = V(b_cf, 0, F32, 128)
        invf = V(b_cf, 128 * 4, F32, 1)
        sign = V(b_cf, 129 * 4, F32, 1)
        halo = [V(b_halo, cc * 64, BF16, 30) for cc in range(8)]
        scb = V(b_scb, 0, BF16, 16)

        W = WStream(P, slots)
        ring_i = [0]
        tf_i = [0]

        def nring():
            r = ring[ring_i[0] % 4]
            ring_i[0] += 1
            return r

        def ntf():
            r = tmpf[tf_i[0] % 2]
            tf_i[0] += 1
            return r

        def mm(out, lhsT, rhs, start, stop):
            P.op("pe", lambda e: e.matmul(out.ap, lhsT.ap, rhs.ap, start=start, stop=stop),
                 R=lhsT.k + rhs.k, W=out.k)

        def act(out, in_, func, bias=None, scale=None):
            R = list(in_.k)
            kw = {}
            if bias is not None:
                if isinstance(bias, V):
                    kw["bias"] = bias.ap
                    R += bias.k
                else:
                    kw["bias"] = bias
            if scale is not None:
                if isinstance(scale, V):
                    kw["scale"] = scale.ap
                    R += scale.k
                else:
                    kw["scale"] = scale
            P.op("act", lambda e: e.activation(out.ap, in_.ap, func, **kw), R=R, W=out.k)

        def tt(out, a, b, op, eng="dve"):
            P.op(eng, lambda e: e.tensor_tensor(out.ap, a.ap, b.ap, op), R=a.k + b.k, W=out.k)

        def ts(out, a, s1, op0, s2=None, op1=None, eng="dve"):
            R = list(a.k)
            a1 = s1
            if isinstance(s1, V):
                R += s1.k
                a1 = s1.ap
            a2 = s2
            if isinstance(s2, V):
                R += s2.k
                a2 = s2.ap
            if op1 is None:
                P.op(eng, lambda e: e.tensor_scalar(out.ap, a.ap, a1, None, op0), R=R, W=out.k)
            else:
                P.op(eng, lambda e: e.tensor_scalar(out.ap, a.ap, a1, a2, op0, op1), R=R, W=out.k)

        def stt(out, a, s, b, op0, op1):
            R = a.k + b.k
            a1 = s
            if isinstance(s, V):
                R = R + s.k
                a1 = s.ap
            P.op("dve", lambda e: e.scalar_tensor_tensor(out.ap, a.ap, a1, b.ap, op0, op1), R=R, W=out.k)

        def cp(out, in_, eng="dve"):
            if eng == "act":
                P.op(eng, lambda e: e.activation(out.ap, in_.ap, AF.Identity), R=in_.k, W=out.k)
            else:
                P.op(eng, lambda e: e.tensor_copy(out.ap, in_.ap), R=in_.k, W=out.k)

        def recip(out, in_):
            P.op("dve", lambda e: e.reciprocal(out.ap, in_.ap), R=in_.k, W=out.k)

        JUNK_BANK = 5
        junk_rhs = V(b_cb, 128 * 2, BF16, 512)

        def junk(n):
            for _ in range(n):
                mm(PS(JUNK_BANK), ones_bf, junk_rhs, True, True)

        def rstd_from_ps(ps, n_feat, out):
            t = ntf()
            act(t, ps, AF.Sqrt, bias=epsb, scale=1.0 / n_feat)
            recip(out, t)

        def sumsq(srcs, psb):
            n = len(srcs)
            for i, s_ in enumerate(srcs):
                r = nring()
                act(r, s_, AF.Square)
                mm(psb, ones_bf, r, i == 0, i == n - 1)

        bgq = []

        def bg_add(stage, fn):
            bgq.append((stage, fn))

        def bg_step(n=1):
            for _ in range(n):
                if bgq:
                    bgq.pop(0)[1]()

        def bg_flush(stage):
            while bgq and bgq[0][0] <= stage:
                bgq.pop(0)[1]()

        MPS_BANK = 6

        def ada_jc(jc):
            mps = PS(MPS_BANK, 144)
            sl = W.next(wada[jc], 2048)
            for kc in range(16):
                mm(mps.sub(jc, jc + 1), sl.sub(kc * 128, kc * 128 + 128), scb.sub(kc, kc + 1), kc == 0, kc == 15)

        def ada_evac(c0, c1):
            mps = PS(MPS_BANK, 144)
            tt(modsb.sub(c0, c1), mps.sub(c0, c1), pv("b_ada", c0, c1 - c0), ALU.add)

        SUBL = [("g_pre_ffn1", "g_post_ffn1", 0.5), ("g_pre_mix", "g_post_mix", 1.0), ("g_pre_ffn2", "g_post_ffn2", 0.5)]

        def ada_gs(s_):
            stt(gs[s_], modsb.sub((3 * s_ + 1) * 16, (3 * s_ + 2) * 16), 1.0, pv(SUBL[s_][0]), ALU.add, ALU.mult)

        def ada_gg(s_):
            stt(gg[s_], modsb.sub((3 * s_ + 2) * 16, (3 * s_ + 3) * 16), SUBL[s_][2], pv(SUBL[s_][1]), ALU.mult, ALU.mult)

        def rope_items(t0, banks):
            X0, X1 = cosf, sinf
            X0i = V(b_cs, 0, I32, T)
            P4, P5, P6 = [PS(b) for b in banks]
            C1 = 6.28125
            C2 = 2.0 * math.pi - C1
            PI_ = 3.1415925
            it = []
            it.append(lambda: P.op("sp", lambda e: e.dma_start(out=X0i.ap, in_=posb[:, t0:t0 + T]), W=X0i.k, dsem="pos"))
            it.append(lambda: cp(P4, X0i))
            it.append(lambda: ts(X1, P4, invf, ALU.mult))
            it.append(lambda: ts(X0i, X1, 1.0 / (2.0 * math.pi), ALU.mult))
            it.append(lambda: cp(X0, X0i))
            it.append(lambda: stt(P5, X0, -C1, X1, ALU.mult, ALU.add))
            it.append(lambda: stt(P4, X0, -C2, P5, ALU.mult, ALU.add))
            it.append(lambda: ts(P6, P4, math.pi / 2.0, ALU.add))

            def wrap(y):
                it.append(lambda: ts(X0, y, math.pi, ALU.is_gt, -2.0 * math.pi, ALU.mult))
                it.append(lambda: tt(X1, y, X0, ALU.add))
                it.append(lambda: ts(X0, X1, -math.pi, ALU.is_lt, 2.0 * math.pi, ALU.mult))
                it.append(lambda: tt(y, X1, X0, ALU.add))
                it.append(lambda: ts(y, y, PI_, ALU.min, -PI_, ALU.max))
            wrap(P4)
            wrap(P6)
            it.append(lambda: act(X1, P4, AF.Sin))
            it.append(lambda: act(X0, P6, AF.Sin))
            it.append(lambda: ts(X1, X1, sign, ALU.mult))
            return it

        def body():
            ring_i[0] = 0
            tf_i[0] = 0
            del bgq[:]
            P.op("sp", lambda e: e.dma_start(out=b_pv.t[:, :], in_=pvec_d), W=V(b_pv, 0, F32, NPV).k, dsem="par0")
            ctmp = V(b_h, 0, F32, NCONST)
            P.op("sp", lambda e: e.dma_start(out=ctmp.ap, in_=const_d), W=ctmp.k, dsem="par1")
            cp(ident_bf, ctmp.sub(C_ID, C_ID + 128))
            cp(V(b_cb, 128 * 2, BF16, 896), ctmp.sub(C_MASK, C_MASK + 896))
            cp(ident_f, ctmp.sub(C_ID, C_ID + 128))
            cp(V(b_cf, 128 * 4, F32, 2), ctmp.sub(C_INVF, C_INVF + 2))
            P.op("dve", lambda e: e.memset(ones_bf.ap, 1.0), W=ones_bf.k)
            P.op("dve", lambda e: e.memset(epsb.ap, EPS), W=epsb.k)
            P.op("dve", lambda e: e.memset(V(b_halo, 0, BF16, 256).ap, 0.0), W=V(b_halo, 0, BF16, 256).k)
            act(scb, pv("c"), AF.Silu)
            load_x(0)
            for f in rope_items(0, (3, 4, 5)):
                f()
            for jc in range(32):
                ada_jc(jc)
            ada_evac(0, 32)
            ada_gs(0)
            for jc in range(32, 48):
                bg_add(0, lambda jc=jc: ada_jc(jc))
            bg_add(0, lambda: (ada_evac(32, 48), ada_gg(0)))
            for jc in range(48, 96):
                bg_add(1, lambda jc=jc: ada_jc(jc))
            bg_add(1, lambda: (ada_evac(48, 96), ada_gs(1), ada_gg(1)))
            for jc in range(96, 144):
                bg_add(2, lambda jc=jc: ada_jc(jc))
            bg_add(2, lambda: (ada_evac(96, 144), ada_gs(2), ada_gg(2)))

            for tt_i in range(NT):
                t0 = tt_i * T
                if tt_i > 0:
                    for c in range(NDC):
                        bg_add(3, lambda c=c: cp(xres[c], xs[c], eng="act"))
                    for f in rope_items(t0, (4, 5, 6)):
                        bg_add(3, f)
                ffn(0, 0, last=False, tt_i=tt_i)
                bg_flush(3)
                mixer(tt_i)
                bg_flush(9)
                ffn(2, 1, last=True, tt_i=tt_i)
            P.op("sp", None, R=[("outT", i, g) for i in range(NT) for g in range(4)])

        XS0 = 12288
        xs = [V(b_h, XS0 + c * 2048, F32, T) for c in range(NDC)]

        def load_x(tt_i, only=None):
            t0 = tt_i * T
            for g in range(4):
                if only is not None and g != only:
                    continue
                if tt_i == 0:
                    xv = V(b_x, g * 4 * T * 4, F32, 4 * T)
                else:
                    xv = V(b_h, XS0 + g * 4 * T * 4, F32, 4 * T)
                P.op("sp", lambda e, t0=t0, g=g, xv=xv: e.dma_start(
                    out=xv.as3(4).ap,
                    in_=xT[g * 512:(g + 1) * 512, :].rearrange("(c p) t -> p c t", p=128)[:, :, t0:t0 + T]),
                    W=xv.k, dsem="xld%d" % g)

        def store_x(tt_i, g):
            t0 = tt_i * T
            xv = V(b_x, g * 4 * T * 4, F32, 4 * T)
            P.op("sp", lambda e: e.dma_start(
                out=outT[g * 512:(g + 1) * 512, :].rearrange("(c p) t -> p c t", p=128)[:, :, t0:t0 + T],
                in_=xv.as3(4).ap), R=xv.k, W=[("outT", tt_i, g)], dsem="xst%d" % g)

        def prenorm(s_):
            sumsq(xres, PS(7))
            rstd_from_ps(PS(7), D, rstd[0])
            for c in range(NDC):
                t = ntf()
                stt(t, xres[c], gs[s_].sub(c, c + 1), rstd[0], ALU.mult, ALU.mult)
                act(u[c], t, AF.Identity, bias=shiftv(s_, c))

        def postnorm(s_, ybf, store_tile=None):
            junk(16)
            rstd_from_ps(PS(7), D, rstd[1])
            for c in range(NDC):
                t = ntf()
                stt(t, ybf[c], gg[s_].sub(c, c + 1), rstd[1], ALU.mult, ALU.mult)
                tt(xres[c], xres[c], t, ALU.add)
                if store_tile is not None and c % 4 == 3:
                    store_x(store_tile, c // 4)

        def ffn(s_, wi, last, tt_i):
            first = (tt_i == 0)
            v = u
            A = rstd[0]
            BB = 4
            shb = V(b_shb, wi * 32, BF16, 16)
            if first:
                cp(shb, modsb.sub(3 * s_ * 16, 3 * s_ * 16 + 16))
            def bias_and_evac(fc, sg_, su_, pg, pu):
                bs2 = V(b_bs, (wi * NFC + fc) * 8, F32, 2)
                if first:
                    pb2 = PS(BB, 2, c0=2 * fc)
                    for kc in range(16):
                        mm(pb2.sub(0, 1), sg_.sub(kc * 128, kc * 128 + 128), shb.sub(kc, kc + 1), kc == 0, kc == 15)
                    for kc in range(16):
                        mm(pb2.sub(1, 2), su_.sub(kc * 128, kc * 128 + 128), shb.sub(kc, kc + 1), kc == 0, kc == 15)
                    cp(bs2, pb2)
                g1 = ntf()
                tt(g1, pg, A, ALU.mult)
                act(g1, g1, AF.Silu, bias=bs2.sub(0, 1))
                u1 = ntf()
                tt(u1, pu, A, ALU.mult)
                stt(hh[fc], u1, bs2.sub(1, 2), g1, ALU.add, ALU.mult)
                bg_step(1)

            W.look = 3
            s4 = [W.next(wg[wi][0], 2048), W.next(wu[wi][0], 2048), W.next(wg[wi][1], 2048), W.next(wu[wi][1], 2048)]
            xin = xs if (s_ == 0 and tt_i > 0) else xres
            for kc in range(16):
                act(v[kc], xin[kc], AF.Identity, scale=gs[s_].sub(kc, kc + 1))
                r = nring()
                act(r, xin[kc], AF.Square)
                mm(PS(7), ones_bf, r, kc == 0, kc == 15)
                for q_ in range(4):
                    mm(PS(q_), s4[q_].sub(kc * 128, kc * 128 + 128), v[kc], kc == 0, kc == 15)
            rstd_from_ps(PS(7), D, A)
            bias_and_evac(0, s4[0], s4[1], PS(0), PS(1))
            bias_and_evac(1, s4[2], s4[3], PS(2), PS(3))
            W.look = 1
            for fc in range(2, NFC):
                sg_ = W.next(wg[wi][fc], 2048)
                su_ = W.next(wu[wi][fc], 2048)
                pg, pu = (PS(0), PS(1)) if fc % 2 == 0 else (PS(2), PS(3))
                for kc in range(16):
                    mm(pg, sg_.sub(kc * 128, kc * 128 + 128), v[kc], kc == 0, kc == 15)
                for kc in range(16):
                    mm(pu, su_.sub(kc * 128, kc * 128 + 128), v[kc], kc == 0, kc == 15)
                bias_and_evac(fc, sg_, su_, pg, pu)
            ybf = u
            pend = []
            for dg in range(4):
                for fg in range(11):
                    sl = W.next(wd[wi][dg, fg], 2048)
                    for fi in range(4):
                        fc = fg * 4 + fi
                        for di in range(4):
                            mm(PS(di), sl.sub(fi * 512 + di * 128, fi * 512 + di * 128 + 128), hh[fc],
                               fc == 0, fc == NFC - 1)
                    if fg == 0 and pend:
                        for r_, idx in pend:
                            mm(PS(7), ones_bf, r_, idx == 0, idx == NDC - 1)
                        pend = []
                    if s_ == 0:
                        bg_step(1)
                for di in range(4):
                    cp(ybf[dg * 4 + di], PS(di))
                    r_ = nring()
                    act(r_, ybf[dg * 4 + di], AF.Square)
                    pend.append((r_, dg * 4 + di))
            for r_, idx in pend:
                mm(PS(7), ones_bf, r_, idx == 0, idx == NDC - 1)
            if s_ == 0:
                bg_flush(1)
            if last and tt_i + 1 < NT:
                load_x(tt_i + 1)
            postnorm(s_, ybf, store_tile=tt_i if last else None)

        def mixer(tt_i):
            t0 = tt_i * T
            HG = 1280
            hglu = [V(b_h, cc * HG, BF16, 640) for cc in range(8)]
            A0 = 10240
            qlat = [V(b_h, A0 + c * 2048, F32, T) for c in range(6)]
            kvlat = [V(b_h, A0 + 12288 + c * 2048, F32, T) for c in range(4)]
            convo = [V(b_h, A0 + c * 2048, F32, T) for c in range(8)]
            qropeZ = [V(b_h, i * 1024, BF16, T) for i in range(8)]
            qn = [V(b_h, 30720 + c * 1024, BF16, T) for c in range(6)]
            ckv = [V(b_h, 36864 + c * 1024, BF16, T) for c in range(4)]
            qnope = [V(b_h, 36864 + i * 1024, BF16, T) for i in range(8)]
            attn = convo
            ymix = [V(b_h, c * 1024, BF16, T) for c in range(16)]
            merged = u
            dgb = [V(b_uy, i * 8192, BF16, 31 * 128) for i in range(2)]

            v = u
            A = rstd[0]
            BBm = 5
            first = (tt_i == 0)
            shb1 = V(b_shb, 64, BF16, 16)
            if first:
                cp(shb1, modsb.sub(48, 64))

            def bsm(i):
                return V(b_bsm, i * 4, F32, 1)

            def bias_col(slab, col):
                pb_ = PS(BBm, 1, c0=col)
                for kc in range(16):
                    mm(pb_, slab.sub(kc * 128, kc * 128 + 128), shb1.sub(kc, kc + 1), kc == 0, kc == 15)
                cp(bsm(col), pb_)

            for cc in range(8):
                cp(hglu[cc].sub(0, 30), halo[cc])
            for cc in range(8):
                sv = W.next(win[cc], 2048)
                sgt = W.next(win[8 + cc], 2048)
                pg, pu = (PS(0), PS(1)) if cc % 2 == 0 else (PS(2), PS(3))
                if cc == 0:
                    for kc in range(16):
                        act(v[kc], xres[kc], AF.Identity, scale=gs[1].sub(kc, kc + 1))
                        r = nring()
                        act(r, xres[kc], AF.Square)
                        mm(PS(7), ones_bf, r, kc == 0, kc == 15)
                        mm(pg, sv.sub(kc * 128, kc * 128 + 128), v[kc], kc == 0, kc == 15)
                    rstd_from_ps(PS(7), D, A)
                else:
                    for kc in range(16):
                        mm(pg, sv.sub(kc * 128, kc * 128 + 128), v[kc], kc == 0, kc == 15)
                for kc in range(16):
                    mm(pu, sgt.sub(kc * 128, kc * 128 + 128), v[kc], kc == 0, kc == 15)
                if first:
                    bias_col(sv, cc)
                    bias_col(sgt, 8 + cc)
                g1 = ntf()
                tt(g1, pu, A, ALU.mult)
                act(g1, g1, AF.Sigmoid, bias=bsm(8 + cc))
                u1 = ntf()
                tt(u1, pg, A, ALU.mult)
                stt(hglu[cc].sub(30, 30 + T), u1, bsm(cc), g1, ALU.add, ALU.mult)
                bg_step(1)
            for c in range(6):
                sl = W.next(win[16 + c], 2048)
                pb = PS(c % 4)
                for kc in range(16):
                    mm(pb, sl.sub(kc * 128, kc * 128 + 128), v[kc], kc == 0, kc == 15)
                if first:
                    bias_col(sl, 16 + c)
                t = ntf()
                tt(t, pb, A, ALU.mult)
                act(qlat[c], t, AF.Identity, bias=bsm(16 + c))
                bg_step(1)
            for c in range(4):
                sl = W.next(win[22 + c], 2048)
                pb = PS((c + 2) % 4)
                for kc in range(16):
                    mm(pb, sl.sub(kc * 128, kc * 128 + 128), v[kc], kc == 0, kc == 15)
                if first:
                    bias_col(sl, 22 + c)
                t = ntf()
                tt(t, pb, A, ALU.mult)
                act(kvlat[c], t, AF.Identity, bias=bsm(22 + c))
                bg_step(1)
            s_raw = W.next(winr[0], 2048)
            s_prm = W.next(winr[1], 2048)
            for kc in range(16):
                mm(PS(0), s_raw.sub(kc * 128, kc * 128 + 128), v[kc], kc == 0, kc == 15)
            for kc in range(16):
                mm(PS(1), s_prm.sub(kc * 128, kc * 128 + 128), v[kc], kc == 0, kc == 15)
            if first:
                bias_col(s_raw, 26)
                bias_col(s_prm, 27)
            r0 = ntf()
            tt(r0, PS(0), A, ALU.mult)
            act(r0, r0, AF.Identity, bias=bsm(26))
            tt(r0, r0, cosf, ALU.mult)
            r1 = ntf()
            tt(r1, PS(1), A, ALU.mult)
            act(r1, r1, AF.Identity, bias=bsm(27))
            tt(r1, r1, sinf, ALU.mult)
            tt(krope.sub(t0, t0 + T), r0, r1, ALU.add)
            bg_step(1)
            def build_diag(cc):
                dg_ = dgb[cc % 2]
                for j in range(31):
                    if j % 2 == 0:
                        ts(dg_.sub(j * 128, j * 128 + 128), ident_f, pv("w_dw", cc * 31 + j, 1), ALU.mult)
                    else:
                        act(dg_.sub(j * 128, j * 128 + 128), ident_f, AF.Identity, scale=pv("w_dw", cc * 31 + j, 1))

            build_diag(0)
            build_diag(1)
            sumsq(kvlat, PS(7))
            sumsq(qlat, PS(3))
            rstd_from_ps(PS(7), 512, rstd[1])
            rstd_from_ps(PS(3), 768, rstd[0])
            for c in range(4):
                stt(ckv[c], kvlat[c], pv("g_kv_lat", c, 1), rstd[1], ALU.mult, ALU.mult)
            for c in range(6):
                stt(qn[c], qlat[c], pv("g_q_lat", c, 1), rstd[0], ALU.mult, ALU.mult)
            ln_pend = []
            for cc in range(8):
                dg_ = dgb[cc % 2]
                if cc >= 2:
                    build_diag(cc)
                pb = PS(cc % 4)
                for j in range(31):
                    mm(pb, dg_.sub(j * 128, j * 128 + 128), hglu[cc].sub(j, j + T), j == 0, j == 30)
                for ra_, rb_, i_ in ln_pend:
                    mm(PS(4), ones_bf, ra_, i_ == 0, i_ == 7)
                    mm(PS(5), ones_bf, rb_, i_ == 0, i_ == 7)
                ln_pend = []
                act(convo[cc], pb, AF.Identity, bias=pv("b_dw", cc, 1))
                ra_ = nring()
                cp(ra_, convo[cc], eng="act")
                rb_ = nring()
                act(rb_, convo[cc], AF.Square)
                ln_pend.append((ra_, rb_, cc))
                cp(halo[cc], hglu[cc].sub(T, T + 30))
                bg_step(2)
            for ra_, rb_, i_ in ln_pend:
                mm(PS(4), ones_bf, ra_, i_ == 0, i_ == 7)
                mm(PS(5), ones_bf, rb_, i_ == 0, i_ == 7)

            LNB = 3
            mu = rstd[0]
            rln = rstd[1]

            ts(mu, PS(4), 1.0 / 1024.0, ALU.mult)
            msq = ntf()
            tt(msq, mu, mu, ALU.mult)
            var = ntf()
            stt(var, PS(5), 1.0 / 1024.0, msq, ALU.mult, ALU.subtract)
            sd = ntf()
            act(sd, var, AF.Sqrt, bias=epsb, scale=1.0)
            recip(rln, sd)

            def ln_c(cc):
                t1 = ntf()
                tt(t1, convo[cc], mu, ALU.subtract)
                t2 = ntf()
                tt(t2, t1, rln, ALU.mult)
                act(convo[cc], t2, AF.Silu, bias=pv("ln_conv_b", cc, 1), scale=pv("ln_conv_g", cc, 1))

            def ln_d(i):
                r = nring()
                act(r, convo[i], AF.Square)
                mm(PS(LNB), ones_bf, r, i == 0, i == 7)
                if i == 7:
                    rstd_from_ps(PS(LNB), 1024, rstd[1])

            def ln_e(cc):
                stt(merged[cc], convo[cc], pv("g_conv_out", cc, 1), rstd[1], ALU.mult, ALU.mult)

            for fn_ in (ln_c, ln_d, ln_e):
                for i in range(8):
                    bg_add(5, lambda fn_=fn_, i=i: fn_(i))

            PB5 = [0, 1, 2, 4, 5]
            pbi = [0]

            def nbank():
                b_ = PB5[pbi[0] % len(PB5)]
                pbi[0] += 1
                return PS(b_)

            for hd in range(NHEAD):
                sl = W.next(wuk[hd], 512)
                pb = nbank()
                for c in range(4):
                    mm(pb, sl.sub(c * 128, c * 128 + 128), ckv[c], c == 0, c == 3)
                cp(knope[hd].sub(t0, t0 + T), pb, eng="act" if (hd % 2 or hd < 4) else "dve")
                if hd >= 2:
                    bg_step(3)
            for half in range(2):
                sl = W.next(wuv[half], 2048)
                for kb in range(4):
                    pb = nbank()
                    for c in range(4):
                        mm(pb, ckv[c].sub(kb * 128, kb * 128 + 128), sl.sub(c * 512, c * 512 + 512), c == 0, c == 3)
                    cp(vc[tt_i * 4 + kb].sub(half * 512, half * 512 + 512), pb, eng="act" if kb % 2 else "dve")
                    bg_step(2)
            qz_all = V(b_h, 0, BF16, 8 * T)
            P.op("dve", lambda e: e.memset(qz_all.ap, 0.0), W=qz_all.k)
            for pr in range(4):
                s_r = W.next(wuqr[pr, 0], 768)
                s_p = W.next(wuqr[pr, 1], 768)
                pr0, pr1 = nbank(), nbank()
                for c in range(6):
                    mm(pr0, s_r.sub(c * 128, c * 128 + 128), qn[c], c == 0, c == 5)
                for c in range(6):
                    mm(pr1, s_p.sub(c * 128, c * 128 + 128), qn[c], c == 0, c == 5)
                t1 = ntf()
                tt(t1, cosf, pr0, ALU.mult)
                t2 = ntf()
                tt(t2, sinf, pr1, ALU.mult)
                tt(qropeZ[2 * pr].part(0, 64), t1.part(0, 64), t2.part(0, 64), ALU.add)
                tt(qropeZ[2 * pr + 1].part(64, 128), t1.part(64, 128), t2.part(64, 128), ALU.add)
                bg_step(2)
            for hd in range(NHEAD):
                sl = W.next(wuqn[hd], 768)
                pb = nbank()
                for c in range(6):
                    mm(pb, sl.sub(c * 128, c * 128 + 128), qn[c], c == 0, c == 5)
                cp(qnope[hd], pb, eng="act" if hd % 2 else "dve")
                bg_step(2)
            bg_flush(5)
            nkb = 4 * tt_i + 4
            units = [(hd, kb) for hd in range(NHEAD) for kb in range(nkb)]
            pts = {}
            LA = 2
            pring = [V(b_uy, 8192 + i * 1024, BF16, T) for i in range(4)]

            def emit_S(i):
                hd, kb = units[i]
                pr, base = hd // 2, 64 * (hd % 2)
                sp_ = PS(i % 3)
                mm(sp_, knope[hd].sub(kb * 128, kb * 128 + 128), qnope[hd], True, False)
                mm(sp_, krope.sub(kb * 128, kb * 128 + 128), qropeZ[hd], False, True)
                pT = pring[i % 4]
                act(pT, sp_, AF.Exp, scale=SCALE)
                if kb >= 4 * tt_i:
                    tt(pT, pT, masks_bf[kb - 4 * tt_i], ALU.mult)
                pts[i] = pT

            def emit_PV(i):
                hd, kb = units[i]
                ob, sb_ = (PS(4), PS(5)) if hd % 2 == 0 else (PS(6), PS(7))
                pT = pts.pop(i)
                mm(ob, vc[kb].sub(hd * 128, hd * 128 + 128), pT, kb == 0, kb == nkb - 1)
                mm(sb_, ones_bf, pT, kb == 0, kb == nkb - 1)
                if kb == nkb - 1:
                    rs_ = ntf()
                    recip(rs_, sb_)
                    tt(attn[hd], rs_, ob, ALU.mult)

            for i in range(len(units) + LA):
                if i < len(units):
                    emit_S(i)
                if i >= LA:
                    emit_PV(i - LA)
            bg_flush(5)
            sumsq(attn, PS(7))
            rstd_from_ps(PS(7), 1024, rstd[2])
            for hd in range(NHEAD):
                stt(merged[8 + hd], attn[hd], pv("g_attn_out", hd, 1), rstd[2], ALU.mult, ALU.mult)
            pend = []
            for dc in range(NDC):
                sl = W.next(wout[dc], 2048)
                pb = PS(dc % 4)
                for kc in range(16):
                    mm(pb, sl.sub(kc * 128, kc * 128 + 128), merged[kc], kc == 0, kc == 15)
                cp(ymix[dc], pb)
                r_ = nring()
                act(r_, ymix[dc], AF.Square)
                pend.append((r_, dc))
                if len(pend) == 3:
                    r0_, i0_ = pend.pop(0)
                    mm(PS(7), ones_bf, r0_, i0_ == 0, i0_ == NDC - 1)
            for r_, idx in pend:
                mm(PS(7), ones_bf, r_, idx == 0, idx == NDC - 1)
            postnorm(1, ymix)

        b_eps = sb("epsb", 4)
        epsb = V(b_eps, 0, F32, 1)

        P.dry = True
        body()
        P.dry = False
        W.reset()
        body()
        keys = P.finalize()
        sems = {}
        for k in keys:
            sems[k] = es.enter_context(nc.semaphore("s_%s_%s" % k))
        with nc.Block() as block:
            @block.sync
            def _(e):
                P.emit("sp", e, sems)

            @block.gpsimd
            def _(e):
                P.emit("pool", e, sems)

            @block.tensor
            def _(e):
                P.emit("pe", e, sems)

            @block.scalar
            def _(e):
                P.emit("act", e, sems)

            @block.vector
            def _(e):
                P.emit("dve", e, sems)
    return nc


def _fm(vec, n):
    return np.ascontiguousarray(np.asarray(vec, np.float32).reshape(n, 128).T)


def _tile_A(w, nk):
    K, N = w.shape
    return np.ascontiguousarray(w.reshape(nk, 128, N // 128, 128).transpose(2, 1, 0, 3).reshape(N // 128, 128, nk * 128))


_NC_CACHE = {}


def kernel(**inp):
    f32 = np.float32
    x = np.asarray(inp["x"], f32)
    B = x.shape[0]
    L = 0
    g = lambda n: np.asarray(inp[n], f32)[L]

    wg1 = _tile_A(g("w1_gate"), 16)
    wu1 = _tile_A(g("w1_up"), 16)
    wg2 = _tile_A(g("w2_gate"), 16)
    wu2 = _tile_A(g("w2_up"), 16)

    def tile_down(w):
        return np.ascontiguousarray(w.reshape(11, 4, 128, 4, 512).transpose(3, 0, 2, 1, 4).reshape(4, 11, 128, 2048))
    wd1 = tile_down(g("w1_down"))
    wd2 = tile_down(g("w2_down"))
    w_in = g("w_in")
    win = _tile_A(w_in[:, :3328], 16)
    rr = w_in[:, 3328:3392]
    rp = np.concatenate([rr[:, 32:], rr[:, :32]], axis=1)
    winr = np.stack([_tile_A(np.concatenate([rr, rr], 1), 16)[0], _tile_A(np.concatenate([rp, rp], 1), 16)[0]])
    w_uq = g("w_uq").reshape(768, 8, 192)
    wuqn = np.stack([_tile_A(np.ascontiguousarray(w_uq[:, h, :128]), 6)[0] for h in range(8)])
    wuqr = np.zeros((4, 2, 128, 768), f32)
    for pr in range(4):
        ra, rb = w_uq[:, 2 * pr, 128:], w_uq[:, 2 * pr + 1, 128:]
        raw = np.concatenate([ra, rb], 1)
        prm = np.concatenate([ra[:, 32:], ra[:, :32], rb[:, 32:], rb[:, :32]], 1)
        wuqr[pr, 0] = _tile_A(np.ascontiguousarray(raw), 6)[0]
        wuqr[pr, 1] = _tile_A(np.ascontiguousarray(prm), 6)[0]
    wuk = _tile_A(g("w_uk"), 4)
    w_uv = g("w_uv")
    wuv = np.ascontiguousarray(w_uv.reshape(4, 128, 2, 512).transpose(2, 1, 0, 3).reshape(2, 128, 2048))
    wout = _tile_A(g("w_out"), 16)
    wada = _tile_A(g("w_ada"), 16)

    consts = np.zeros((128, NCONST), f32)
    consts[:, C_ID:C_ID + 128] = np.eye(128, dtype=f32)
    kk = np.arange(128)[:, None]
    qq = np.arange(512)[None, :]
    xx = np.arange(896)[None, :]
    consts[:, C_MASK:C_MASK + 896] = ((xx - 384) >= kk).astype(f32)
    inv_freq = (10000.0 ** (-np.arange(32, dtype=f32) / f32(32))).astype(f32)
    consts[:, C_INVF] = inv_freq[np.arange(128) % 32]
    consts[:, C_SIGN] = np.where((np.arange(128) % 64) < 32, -1.0, 1.0)

    pv_common = np.zeros((128, NPV), f32)
    def put(name, arr):
        o, w = PV[name]
        pv_common[:, o:o + w] = arr
    for n in ["g_pre_ffn1", "g_post_ffn1", "g_pre_mix", "g_post_mix", "g_pre_ffn2", "g_post_ffn2"]:
        put(n, _fm(g(n), 16))
    put("b_ada", _fm(g("b_ada"), 144))
    for n in ["b_dw", "ln_conv_g", "ln_conv_b", "g_conv_out", "g_attn_out"]:
        put(n, _fm(g(n), 8))
    put("g_q_lat", _fm(g("g_q_lat"), 6))
    put("g_kv_lat", _fm(g("g_kv_lat"), 4))
    wdw = g("w_dw")
    put("w_dw", np.ascontiguousarray(wdw.reshape(31, 8, 128).transpose(2, 1, 0).reshape(128, 248)))

    c = np.asarray(inp["c"], f32)
    pos = np.asarray(inp["positions"], np.int32)
    in_maps = []
    for b in range(B):
        pvb = pv_common.copy()
        o, w = PV["c"]
        pvb[:, o:o + w] = _fm(c[b], 16)
        in_maps.append(dict(
            xT=np.ascontiguousarray(x[b].T), posb=np.ascontiguousarray(np.broadcast_to(pos[b][None, :], (128, S))),
            pvec=pvb, consts=consts, wg1=wg1, wu1=wu1, wd1=wd1, wg2=wg2, wu2=wu2, wd2=wd2, win=win, winr=winr,
            wuqn=wuqn, wuqr=wuqr, wuk=wuk, wuv=wuv, wout=wout, wada=wada))
    if "nc" not in _NC_CACHE:
        _NC_CACHE["nc"] = build_program()
    nc = _NC_CACHE["nc"]
    res = run_bass_kernel_spmd(nc, in_maps, core_ids=list(range(B)))
    out = np.stack([np.ascontiguousarray(res.results[b]["outT"].T) for b in range(B)]).astype(f32)
    return out
```

```python
import math
import numpy as np
import concourse.bass as bass
import concourse.mybir as mybir
from concourse.bass_utils import run_bass_kernel_spmd

F32 = mybir.dt.float32
BF16 = mybir.dt.bfloat16
I32 = mybir.dt.int32
AF = mybir.ActivationFunctionType
ALU = mybir.AluOpType

D = 2048
S = 2048
T = 512
NT = S // T
DFF = 5632
NFC = DFF // 128
NDC = D // 128
EPS = 1e-6
NHEAD = 8
SCALE = (128 + 64) ** -0.5
NSLOT = 6
GRAN = 256

PV = {}
_off = 0
for _n, _w in [("g_pre_ffn1", 16), ("g_post_ffn1", 16), ("g_pre_mix", 16), ("g_post_mix", 16),
               ("g_pre_ffn2", 16), ("g_post_ffn2", 16), ("b_ada", 144), ("b_dw", 8), ("ln_conv_g", 8),
               ("ln_conv_b", 8), ("g_conv_out", 8), ("g_attn_out", 8), ("g_q_lat", 6), ("g_kv_lat", 4),
               ("w_dw", 8 * 31), ("c", 16)]:
    PV[_n] = (_off, _w)
    _off += _w
NPV = _off
C_ID, C_MASK, C_INVF, C_SIGN, NCONST = 0, 128, 128 + 896, 128 + 896 + 1, 128 + 896 + 2


class V:
    def __init__(self, buf, off, dtype, n, p0=0, p1=128, shape=None):
        self.buf, self.off, self.dtype, self.n, self.p0, self.p1, self.shape = buf, off, dtype, n, p0, p1, shape
        self.sz = 4 if dtype in (F32, I32) else 2

    @property
    def ap(self):
        t = self.buf.t if self.dtype == self.buf.dtype else self.buf.t.bitcast(self.dtype)
        e0 = self.off // self.sz
        a = t[self.p0:self.p1, e0:e0 + self.n]
        if self.shape is not None:
            a = a.rearrange("p (a b) -> p a b", a=self.shape[0])
        return a

    @property
    def k(self):
        if self.buf.name.startswith("ps"):
            return [(self.buf.name, 0)]
        g0 = self.off // GRAN
        g1 = (self.off + self.n * self.sz - 1) // GRAN
        return [(self.buf.name, g) for g in range(g0, g1 + 1)]

    def sub(self, a, b):
        return V(self.buf, self.off + a * self.sz, self.dtype, b - a, self.p0, self.p1)

    def part(self, p0, p1):
        return V(self.buf, self.off, self.dtype, self.n, p0, p1)

    def as3(self, a):
        return V(self.buf, self.off, self.dtype, self.n, self.p0, self.p1, shape=(a, self.n // a))


class Buf:
    def __init__(self, name, t, dtype):
        self.name, self.t, self.dtype = name, t, dtype


class Prog:
    def __init__(self):
        self.ops = []
        self.res = {}
        self.dry = False

    def op(self, eng, fn, R=(), W=(), dsem=None):
        if self.dry:
            return None
        idx = len(self.ops)
        deps = set()
        for r in R:
            st = self.res.get(r)
            if st is None:
                st = self.res[r] = [None, []]
            if st[0] is not None:
                deps.add(st[0])
        for w in W:
            st = self.res.get(w)
            if st is None:
                st = self.res[w] = [None, []]
            if st[0] is not None:
                deps.add(st[0])
            deps.update(st[1])
        for r in R:
            self.res[r][1].append(idx)
        for w in W:
            st = self.res[w]
            st[0] = idx
            st[1] = []
        deps.discard(idx)
        self.ops.append(dict(eng=eng, fn=fn, deps=deps, dsem=dsem, need=False, sig=None))
        return idx

    def finalize(self):
        ops = self.ops
        for o in ops:
            for d in o["deps"]:
                if ops[d]["eng"] == "pe" and o["eng"] == "pe":
                    continue
                ops[d]["need"] = True
        cnt = {}
        for o in ops:
            if o["dsem"] is not None:
                key = ("dma", o["dsem"])
                cnt[key] = cnt.get(key, 0) + 16
                o["sig"] = (key, cnt[key])
            elif o["need"]:
                key = ("eng", o["eng"])
                cnt[key] = cnt.get(key, 0) + 1
                o["sig"] = (key, cnt[key])
        return sorted(cnt.keys())

    def emit(self, eng_name, e, sems):
        ops = self.ops
        waited = {}
        for o in ops:
            if o["eng"] != eng_name:
                continue
            need = {}
            for d in o["deps"]:
                od = ops[d]
                if od["eng"] == "pe" and eng_name == "pe":
                    continue
                key, val = od["sig"]
                if need.get(key, 0) < val:
                    need[key] = val
            for key, val in need.items():
                if waited.get(key, 0) < val:
                    e.wait_ge(sems[key], val)
                    waited[key] = val
            if o["fn"] is None:
                continue
            ins = o["fn"](e)
            if o["sig"] is not None:
                key, _ = o["sig"]
                ins.then_inc(sems[key], 16 if key[0] == "dma" else 1)


class WStream:
    def __init__(self, P, slots):
        self.P, self.slots = P, slots
        self.seq = []
        self.i = 0
        self.issued = 0
        self.look = 1

    def reset(self):
        self.i = 0
        self.issued = 0

    def _issue(self, j):
        src, ncols = self.seq[j]
        sl = self.slots[j % NSLOT].sub(0, ncols)
        self.P.op("pool", lambda e, s=src, d=sl: e.dma_start(out=d.ap, in_=s), R=[], W=sl.k,
                  dsem="slot%d" % (j % NSLOT))

    def next(self, src, ncols):
        if self.P.dry:
            self.seq.append((src, ncols))
            j = self.i
            self.i += 1
            return self.slots[j % NSLOT].sub(0, ncols)
        j = self.i
        while self.issued < min(len(self.seq), j + NSLOT - self.look):
            self._issue(self.issued)
            self.issued += 1
        self.i += 1
        return self.slots[j % NSLOT].sub(0, ncols)


def build_program():
    nc = bass.Bass("TRN2", target_bir_lowering=False)
    dr = {}

    def din(name, shape, dt=F32):
        dr[name] = nc.dram_tensor(name, shape, dt, kind="ExternalInput").ap()
        return dr[name]

    xT = din("xT", [D, S])
    posb = din("posb", [128, S], I32)
    pvec_d = din("pvec", [128, NPV])
    const_d = din("consts", [128, NCONST])
    wg = [din("wg1", [NFC, 128, 2048]), din("wg2", [NFC, 128, 2048])]
    wu = [din("wu1", [NFC, 128, 2048]), din("wu2", [NFC, 128, 2048])]
    wd = [din("wd1", [4, 11, 128, 2048]), din("wd2", [4, 11, 128, 2048])]
    win = din("win", [26, 128, 2048])
    winr = din("winr", [2, 128, 2048])
    wuqn = din("wuqn", [8, 128, 768])
    wuqr = din("wuqr", [4, 2, 128, 768])
    wuk = din("wuk", [8, 128, 512])
    wuv = din("wuv", [2, 128, 2048])
    wout = din("wout", [16, 128, 2048])
    wada = din("wada", [144, 128, 2048])
    outT = nc.dram_tensor("outT", [D, S], F32, kind="ExternalOutput").ap()

    P = Prog()
    import contextlib
    with contextlib.ExitStack() as es:
        def sb(name, nbytes, dt=F32):
            t = es.enter_context(nc.sbuf_tensor("sb_" + name, [128, nbytes // (4 if dt in (F32, I32) else 2)], dt))
            return Buf(name, t, dt)

        b_x = sb("xres", NDC * T * 4)
        b_uy = sb("uy", NDC * T * 2, BF16)
        b_h = sb("h", NFC * T * 2, BF16)
        b_kn = sb("knope", NHEAD * S * 2, BF16)
        b_v = sb("vc", 16 * 1024 * 2, BF16)
        b_kr = sb("krope", S * 2, BF16)
        b_sl = sb("slots", NSLOT * 4096, BF16)
        b_ring = sb("ring", 4 * 1024, BF16)
        b_tf = sb("tmpf", 2 * 2048)
        b_rs = sb("rstd", 2 * 2048)
        b_cs = sb("cossin", 2 * 2048)
        b_pv = sb("pvec", NPV * 4)
        b_md = sb("modsb", (144 + 9 * 16) * 4)
        b_cb = sb("constb", (128 + 896 + 128) * 2, BF16)
        b_cf = sb("constf", 130 * 4)
        b_halo = sb("halo", 8 * 32 * 2, BF16)
        b_scb = sb("scb", 16 * 2, BF16)
        b_bs = sb("bsb", 2 * NFC * 2 * 4)
        b_shb = sb("shb", 3 * 16 * 2, BF16)
        b_bsm = sb("bsm", 28 * 4)
        pst = []
        for b in range(8):
            t = es.enter_context(nc.psum_tensor("ps%d" % b, [128, 512], F32))
            pst.append(Buf("ps%d" % b, t, F32))

        def PS(b, n=512, c0=0, p0=0, p1=128):
            return V(pst[b], c0 * 4, F32, n, p0, p1)

        xres = [V(b_x, c * T * 4, F32, T) for c in range(NDC)]
        xres_all = V(b_x, 0, F32, NDC * T)
        u = [V(b_uy, c * T * 2, BF16, T) for c in range(NDC)]
        hh = [V(b_h, c * T * 2, BF16, T) for c in range(NFC)]
        knope = [V(b_kn, hd * S * 2, BF16, S) for hd in range(NHEAD)]
        vc = [V(b_v, kb * 1024 * 2, BF16, 1024) for kb in range(16)]
        krope = V(b_kr, 0, BF16, S)
        slots = [V(b_sl, i * 4096, BF16, 2048) for i in range(NSLOT)]
        ring = [V(b_ring, i * 1024, BF16, T) for i in range(4)]
        tmpf = [V(b_tf, i * 2048, F32, T) for i in range(2)]
        rstd = [V(b_rs, i * 2048, F32, T) for i in range(2)]
        rstd.append(rstd[0])
        cosf = V(b_cs, 0, F32, T)
        sinf = V(b_cs, 2048, F32, T)

        def pv(name, c0=0, n=None):
            o, w = PV[name]
            n = w - c0 if n is None else n
            return V(b_pv, (o + c0) * 4, F32, n)

        modsb = V(b_md, 0, F32, 144)
        gs = [V(b_md, (144 + s * 16) * 4, F32, 16) for s in range(3)]
        gg = [V(b_md, (144 + 48 + s * 16) * 4, F32, 16) for s in range(3)]
        def shiftv(s, c):
            return modsb.sub((3 * s) * 16 + c, (3 * s) * 16 + c + 1)
        ident_bf = V(b_cb, 0, BF16, 128)
        masks_bf = [V(b_cb, (128 + 384 - 128 * j) * 2, BF16, 512) for j in range(4)]
        ones_bf = V(b_cb, (128 + 896) * 2, BF16, 128)
        ident_f = V(b_cf, 0, F32, 128)
        invf = V(b_cf, 128 * 4, F32, 1)
        sign = V(b_cf, 129 * 4, F32, 1)
        halo = [V(b_halo, cc * 64, BF16, 30) for cc in range(8)]
        scb = V(b_scb, 0, BF16, 16)

        W = WStream(P, slots)
        ring_i = [0]
        tf_i = [0]

        def nring():
            r = ring[ring_i[0] % 4]
            ring_i[0] += 1
            return r

        def ntf():
            r = tmpf[tf_i[0] % 2]
            tf_i[0] += 1
            return r

        def mm(out, lhsT, rhs, start, stop):
            P.op("pe", lambda e: e.matmul(out.ap, lhsT.ap, rhs.ap, start=start, stop=stop),
                 R=lhsT.k + rhs.k, W=out.k)

        def act(out, in_, func, bias=None, scale=None):
            R = list(in_.k)
            kw = {}
            if bias is not None:
                if isinstance(bias, V):
                    kw["bias"] = bias.ap
                    R += bias.k
                else:
                    kw["bias"] = bias
            if scale is not None:
                if isinstance(scale, V):
                    kw["scale"] = scale.ap
                    R += scale.k
                else:
                    kw["scale"] = scale
            P.op("act", lambda e: e.activation(out.ap, in_.ap, func, **kw), R=R, W=out.k)

        def tt(out, a, b, op, eng="dve"):
            P.op(eng, lambda e: e.tensor_tensor(out.ap, a.ap, b.ap, op), R=a.k + b.k, W=out.k)

        def ts(out, a, s1, op0, s2=None, op1=None, eng="dve"):
            R = list(a.k)
            a1 = s1
            if isinstance(s1, V):
                R += s1.k
                a1 = s1.ap
            a2 = s2
            if isinstance(s2, V):
                R += s2.k
                a2 = s2.ap
            if op1 is None:
                P.op(eng, lambda e: e.tensor_scalar(out.ap, a.ap, a1, None, op0), R=R, W=out.k)
            else:
                P.op(eng, lambda e: e.tensor_scalar(out.ap, a.ap, a1, a2, op0, op1), R=R, W=out.k)

        def stt(out, a, s, b, op0, op1):
            R = a.k + b.k
            a1 = s
            if isinstance(s, V):
                R = R + s.k
                a1 = s.ap
            P.op("dve", lambda e: e.scalar_tensor_tensor(out.ap, a.ap, a1, b.ap, op0, op1), R=R, W=out.k)

        def cp(out, in_, eng="dve"):
            if eng == "act":
                P.op(eng, lambda e: e.activation(out.ap, in_.ap, AF.Identity), R=in_.k, W=out.k)
            else:
                P.op(eng, lambda e: e.tensor_copy(out.ap, in_.ap), R=in_.k, W=out.k)

        def recip(out, in_):
            P.op("dve", lambda e: e.reciprocal(out.ap, in_.ap), R=in_.k, W=out.k)

        JUNK_BANK = 5
        junk_rhs = V(b_cb, 128 * 2, BF16, 512)

        def junk(n):
            for _ in range(n):
                mm(PS(JUNK_BANK), ones_bf, junk_rhs, True, True)

        def rstd_from_ps(ps, n_feat, out):
            t = ntf()
            act(t, ps, AF.Sqrt, bias=epsb, scale=1.0 / n_feat)
            recip(out, t)

        def sumsq(srcs, psb):
            n = len(srcs)
            for i, s_ in enumerate(srcs):
                r = nring()
                act(r, s_, AF.Square)
                mm(psb, ones_bf, r, i == 0, i == n - 1)

        bgq = []

        def bg_add(stage, fn):
            bgq.append((stage, fn))

        def bg_step(n=1):
            for _ in range(n):
                if bgq:
                    bgq.pop(0)[1]()

        def bg_flush(stage):
            while bgq and bgq[0][0] <= stage:
                bgq.pop(0)[1]()

        MPS_BANK = 6

        def ada_jc(jc):
            mps = PS(MPS_BANK, 144)
            sl = W.next(wada[jc], 2048)
            for kc in range(16):
                mm(mps.sub(jc, jc + 1), sl.sub(kc * 128, kc * 128 + 128), scb.sub(kc, kc + 1), kc == 0, kc == 15)

        def ada_evac(c0, c1):
            mps = PS(MPS_BANK, 144)
            tt(modsb.sub(c0, c1), mps.sub(c0, c1), pv("b_ada", c0, c1 - c0), ALU.add)

        SUBL = [("g_pre_ffn1", "g_post_ffn1", 0.5), ("g_pre_mix", "g_post_mix", 1.0), ("g_pre_ffn2", "g_post_ffn2", 0.5)]

        def ada_gs(s_):
            stt(gs[s_], modsb.sub((3 * s_ + 1) * 16, (3 * s_ + 2) * 16), 1.0, pv(SUBL[s_][0]), ALU.add, ALU.mult)

        def ada_gg(s_):
            stt(gg[s_], modsb.sub((3 * s_ + 2) * 16, (3 * s_ + 3) * 16), SUBL[s_][2], pv(SUBL[s_][1]), ALU.mult, ALU.mult)

        def rope_items(t0, banks):
            X0, X1 = cosf, sinf
            X0i = V(b_cs, 0, I32, T)
            P4, P5, P6 = [PS(b) for b in banks]
            C1 = 6.28125
            C2 = 2.0 * math.pi - C1
            PI_ = 3.1415925
            it = []
            it.append(lambda: P.op("sp", lambda e: e.dma_start(out=X0i.ap, in_=posb[:, t0:t0 + T]), W=X0i.k, dsem="pos"))
            it.append(lambda: cp(P4, X0i))
            it.append(lambda: ts(X1, P4, invf, ALU.mult))
            it.append(lambda: ts(X0i, X1, 1.0 / (2.0 * math.pi), ALU.mult))
            it.append(lambda: cp(X0, X0i))
            it.append(lambda: stt(P5, X0, -C1, X1, ALU.mult, ALU.add))
            it.append(lambda: stt(P4, X0, -C2, P5, ALU.mult, ALU.add))
            it.append(lambda: ts(P6, P4, math.pi / 2.0, ALU.add))

            def wrap(y):
                it.append(lambda: ts(X0, y, math.pi, ALU.is_gt, -2.0 * math.pi, ALU.mult))
                it.append(lambda: tt(X1, y, X0, ALU.add))
                it.append(lambda: ts(X0, X1, -math.pi, ALU.is_lt, 2.0 * math.pi, ALU.mult))
                it.append(lambda: tt(y, X1, X0, ALU.add))
                it.append(lambda: ts(y, y, PI_, ALU.min, -PI_, ALU.max))
            wrap(P4)
            wrap(P6)
            it.append(lambda: act(X1, P4, AF.Sin))
            it.append(lambda: act(X0, P6, AF.Sin))
            it.append(lambda: ts(X1, X1, sign, ALU.mult))
            return it

        def body():
            ring_i[0] = 0
            tf_i[0] = 0
            del bgq[:]
            P.op("sp", lambda e: e.dma_start(out=b_pv.t[:, :], in_=pvec_d), W=V(b_pv, 0, F32, NPV).k, dsem="par0")
            ctmp = V(b_h, 0, F32, NCONST)
            P.op("sp", lambda e: e.dma_start(out=ctmp.ap, in_=const_d), W=ctmp.k, dsem="par1")
            cp(ident_bf, ctmp.sub(C_ID, C_ID + 128))
            cp(V(b_cb, 128 * 2, BF16, 896), ctmp.sub(C_MASK, C_MASK + 896))
            cp(ident_f, ctmp.sub(C_ID, C_ID + 128))
            cp(V(b_cf, 128 * 4, F32, 2), ctmp.sub(C_INVF, C_INVF + 2))
            P.op("dve", lambda e: e.memset(ones_bf.ap, 1.0), W=ones_bf.k)
            P.op("dve", lambda e: e.memset(epsb.ap, EPS), W=epsb.k)
            P.op("dve", lambda e: e.memset(V(b_halo, 0, BF16, 256).ap, 0.0), W=V(b_halo, 0, BF16, 256).k)
            act(scb, pv("c"), AF.Silu)
            load_x(0)
            for f in rope_items(0, (3, 4, 5)):
                f()
            for jc in range(32):
                ada_jc(jc)
            ada_evac(0, 32)
            ada_gs(0)
            for jc in range(32, 48):
                bg_add(0, lambda jc=jc: ada_jc(jc))
            bg_add(0, lambda: (ada_evac(32, 48), ada_gg(0)))
            for jc in range(48, 96):
                bg_add(1, lambda jc=jc: ada_jc(jc))
            bg_add(1, lambda: (ada_evac(48, 96), ada_gs(1), ada_gg(1)))
            for jc in range(96, 144):
                bg_add(2, lambda jc=jc: ada_jc(jc))
            bg_add(2, lambda: (ada_evac(96, 144), ada_gs(2), ada_gg(2)))

            for tt_i in range(NT):
                t0 = tt_i * T
                if tt_i > 0:
                    for c in range(NDC):
                        bg_add(3, lambda c=c: cp(xres[c], xs[c], eng="act"))
                    for f in rope_items(t0, (4, 5, 6)):
                        bg_add(3, f)
                ffn(0, 0, last=False, tt_i=tt_i)
                bg_flush(3)
                mixer(tt_i)
                bg_flush(9)
                ffn(2, 1, last=True, tt_i=tt_i)
            P.op("sp", None, R=[("outT", i, g) for i in range(NT) for g in range(4)])

        XS0 = 12288
        xs = [V(b_h, XS0 + c * 2048, F32, T) for c in range(NDC)]

        def load_x(tt_i, only=None):
            t0 = tt_i * T
            for g in range(4):
                if only is not None and g != only:
                    continue
                if tt_i == 0:
                    xv = V(b_x, g * 4 * T * 4, F32, 4 * T)
                else:
                    xv = V(b_h, XS0 + g * 4 * T * 4, F32, 4 * T)
                P.op("sp", lambda e, t0=t0, g=g, xv=xv: e.dma_start(
                    out=xv.as3(4).ap,
                    in_=xT[g * 512:(g + 1) * 512, :].rearrange("(c p) t -> p c t", p=128)[:, :, t0:t0 + T]),
                    W=xv.k, dsem="xld%d" % g)

        def store_x(tt_i, g):
            t0 = tt_i * T
            xv = V(b_x, g * 4 * T * 4, F32, 4 * T)
            P.op("sp", lambda e: e.dma_start(
                out=outT[g * 512:(g + 1) * 512, :].rearrange("(c p) t -> p c t", p=128)[:, :, t0:t0 + T],
                in_=xv.as3(4).ap), R=xv.k, W=[("outT", tt_i, g)], dsem="xst%d" % g)

        def prenorm(s_):
            sumsq(xres, PS(7))
            rstd_from_ps(PS(7), D, rstd[0])
            for c in range(NDC):
                t = ntf()
                stt(t, xres[c], gs[s_].sub(c, c + 1), rstd[0], ALU.mult, ALU.mult)
                act(u[c], t, AF.Identity, bias=shiftv(s_, c))

        def postnorm(s_, ybf, store_tile=None):
            junk(16)
            rstd_from_ps(PS(7), D, rstd[1])
            for c in range(NDC):
                t = ntf()
                stt(t, ybf[c], gg[s_].sub(c, c + 1), rstd[1], ALU.mult, ALU.mult)
                tt(xres[c], xres[c], t, ALU.add)
                if store_tile is not None and c % 4 == 3:
                    store_x(store_tile, c // 4)

        def ffn(s_, wi, last, tt_i):
            first = (tt_i == 0)
            v = u
            A = rstd[0]
            BB = 4
            shb = V(b_shb, wi * 32, BF16, 16)
            if first:
                cp(shb, modsb.sub(3 * s_ * 16, 3 * s_ * 16 + 16))
            def bias_and_evac(fc, sg_, su_, pg, pu):
                bs2 = V(b_bs, (wi * NFC + fc) * 8, F32, 2)
                if first:
                    pb2 = PS(BB, 2, c0=2 * fc)
                    for kc in range(16):
                        mm(pb2.sub(0, 1), sg_.sub(kc * 128, kc * 128 + 128), shb.sub(kc, kc + 1), kc == 0, kc == 15)
                    for kc in range(16):
                        mm(pb2.sub(1, 2), su_.sub(kc * 128, kc * 128 + 128), shb.sub(kc, kc + 1), kc == 0, kc == 15)
                    cp(bs2, pb2)
                g1 = ntf()
                tt(g1, pg, A, ALU.mult)
                act(g1, g1, AF.Silu, bias=bs2.sub(0, 1))
                u1 = ntf()
                tt(u1, pu, A, ALU.mult)
                stt(hh[fc], u1, bs2.sub(1, 2), g1, ALU.add, ALU.mult)
                bg_step(1)

            W.look = 3
            s4 = [W.next(wg[wi][0], 2048), W.next(wu[wi][0], 2048), W.next(wg[wi][1], 2048), W.next(wu[wi][1], 2048)]
            xin = xs if (s_ == 0 and tt_i > 0) else xres
            for kc in range(16):
                act(v[kc], xin[kc], AF.Identity, scale=gs[s_].sub(kc, kc + 1))
                r = nring()
                act(r, xin[kc], AF.Square)
                mm(PS(7), ones_bf, r, kc == 0, kc == 15)
                for q_ in range(4):
                    mm(PS(q_), s4[q_].sub(kc * 128, kc * 128 + 128), v[kc], kc == 0, kc == 15)
            rstd_from_ps(PS(7), D, A)
            bias_and_evac(0, s4[0], s4[1], PS(0), PS(1))
            bias_and_evac(1, s4[2], s4[3], PS(2), PS(3))
            W.look = 1
            for fc in range(2, NFC):
                sg_ = W.next(wg[wi][fc], 2048)
                su_ = W.next(wu[wi][fc], 2048)
                pg, pu = (PS(0), PS(1)) if fc % 2 == 0 else (PS(2), PS(3))
                for kc in range(16):
                    mm(pg, sg_.sub(kc * 128, kc * 128 + 128), v[kc], kc == 0, kc == 15)
                for kc in range(16):
                    mm(pu, su_.sub(kc * 128, kc * 128 + 128), v[kc], kc == 0, kc == 15)
                bias_and_evac(fc, sg_, su_, pg, pu)
            ybf = u
            pend = []
            for dg in range(4):
                for fg in range(11):
                    sl = W.next(wd[wi][dg, fg], 2048)
                    for fi in range(4):
                        fc = fg * 4 + fi
                        for di in range(4):
                            mm(PS(di), sl.sub(fi * 512 + di * 128, fi * 512 + di * 128 + 128), hh[fc],
                               fc == 0, fc == NFC - 1)
                    if fg == 0 and pend:
                        for r_, idx in pend:
                            mm(PS(7), ones_bf, r_, idx == 0, idx == NDC - 1)
                        pend = []
                    if s_ == 0:
                        bg_step(1)
                for di in range(4):
                    cp(ybf[dg * 4 + di], PS(di))
                    r_ = nring()
                    act(r_, ybf[dg * 4 + di], AF.Square)
                    pend.append((r_, dg * 4 + di))
            for r_, idx in pend:
                mm(PS(7), ones_bf, r_, idx == 0, idx == NDC - 1)
            if s_ == 0:
                bg_flush(1)
            if last and tt_i + 1 < NT:
                load_x(tt_i + 1)
            postnorm(s_, ybf, store_tile=tt_i if last else None)

        def mixer(tt_i):
            t0 = tt_i * T
            HG = 1280
            hglu = [V(b_h, cc * HG, BF16, 640) for cc in range(8)]
            A0 = 10240
            qlat = [V(b_h, A0 + c * 2048, F32, T) for c in range(6)]
            kvlat = [V(b_h, A0 + 12288 + c * 2048, F32, T) for c in range(4)]
            convo = [V(b_h, A0 + c * 2048, F32, T) for c in range(8)]
            qropeZ = [V(b_h, i * 1024, BF16, T) for i in range(8)]
            qn = [V(b_h, 30720 + c * 1024, BF16, T) for c in range(6)]
            ckv = [V(b_h, 36864 + c * 1024, BF16, T) for c in range(4)]
            qnope = [V(b_h, 36864 + i * 1024, BF16, T) for i in range(8)]
            attn = convo
            ymix = [V(b_h, c * 1024, BF16, T) for c in range(16)]
            merged = u
            dgb = [V(b_uy, i * 8192, BF16, 31 * 128) for i in range(2)]

            v = u
            A = rstd[0]
            BBm = 5
            first = (tt_i == 0)
            shb1 = V(b_shb, 64, BF16, 16)
            if first:
                cp(shb1, modsb.sub(48, 64))

            def bsm(i):
                return V(b_bsm, i * 4, F32, 1)

            def bias_col(slab, col):
                pb_ = PS(BBm, 1, c0=col)
                for kc in range(16):
                    mm(pb_, slab.sub(kc * 128, kc * 128 + 128), shb1.sub(kc, kc + 1), kc == 0, kc == 15)
                cp(bsm(col), pb_)

            for cc in range(8):
                cp(hglu[cc].sub(0, 30), halo[cc])
            for cc in range(8):
                sv = W.next(win[cc], 2048)
                sgt = W.next(win[8 + cc], 2048)
                pg, pu = (PS(0), PS(1)) if cc % 2 == 0 else (PS(2), PS(3))
                if cc == 0:
                    for kc in range(16):
                        act(v[kc], xres[kc], AF.Identity, scale=gs[1].sub(kc, kc + 1))
                        r = nring()
                        act(r, xres[kc], AF.Square)
                        mm(PS(7), ones_bf, r, kc == 0, kc == 15)
                        mm(pg, sv.sub(kc * 128, kc * 128 + 128), v[kc], kc == 0, kc == 15)
                    rstd_from_ps(PS(7), D, A)
                else:
                    for kc in range(16):
                        mm(pg, sv.sub(kc * 128, kc * 128 + 128), v[kc], kc == 0, kc == 15)
                for kc in range(16):
                    mm(pu, sgt.sub(kc * 128, kc * 128 + 128), v[kc], kc == 0, kc == 15)
                if first:
                    bias_col(sv, cc)
                    bias_col(sgt, 8 + cc)
                g1 = ntf()
                tt(g1, pu, A, ALU.mult)
                act(g1, g1, AF.Sigmoid, bias=bsm(8 + cc))
                u1 = ntf()
                tt(u1, pg, A, ALU.mult)
                stt(hglu[cc].sub(30, 30 + T), u1, bsm(cc), g1, ALU.add, ALU.mult)
                bg_step(1)
            for c in range(6):
                sl = W.next(win[16 + c], 2048)
                pb = PS(c % 4)
                for kc in range(16):
                    mm(pb, sl.sub(kc * 128, kc * 128 + 128), v[kc], kc == 0, kc == 15)
                if first:
                    bias_col(sl, 16 + c)
                t = ntf()
                tt(t, pb, A, ALU.mult)
                act(qlat[c], t, AF.Identity, bias=bsm(16 + c))
                bg_step(1)
            for c in range(4):
                sl = W.next(win[22 + c], 2048)
                pb = PS((c + 2) % 4)
                for kc in range(16):
                    mm(pb, sl.sub(kc * 128, kc * 128 + 128), v[kc], kc == 0, kc == 15)
                if first:
                    bias_col(sl, 22 + c)
                t = ntf()
                tt(t, pb, A, ALU.mult)
                act(kvlat[c], t, AF.Identity, bias=bsm(22 + c))
                bg_step(1)
            s_raw = W.next(winr[0], 2048)
            s_prm = W.next(winr[1], 2048)
            for kc in range(16):
                mm(PS(0), s_raw.sub(kc * 128, kc * 128 + 128), v[kc], kc == 0, kc == 15)
            for kc in range(16):
                mm(PS(1), s_prm.sub(kc * 128, kc * 128 + 128), v[kc], kc == 0, kc == 15)
            if first:
                bias_col(s_raw, 26)
                bias_col(s_prm, 27)
            r0 = ntf()
            tt(r0, PS(0), A, ALU.mult)
            act(r0, r0, AF.Identity, bias=bsm(26))
            tt(r0, r0, cosf, ALU.mult)
            r1 = ntf()
            tt(r1, PS(1), A, ALU.mult)
            act(r1, r1, AF.Identity, bias=bsm(27))
            tt(r1, r1, sinf, ALU.mult)
            tt(krope.sub(t0, t0 + T), r0, r1, ALU.add)
            bg_step(1)
            def build_diag(cc):
                dg_ = dgb[cc % 2]
                for j in range(31):
                    if j % 2 == 0:
                        ts(dg_.sub(j * 128, j * 128 + 128), ident_f, pv("w_dw", cc * 31 + j, 1), ALU.mult)
                    else:
                        act(dg_.sub(j * 128, j * 128 + 128), ident_f, AF.Identity, scale=pv("w_dw", cc * 31 + j, 1))

            build_diag(0)
            build_diag(1)
            sumsq(kvlat, PS(7))
            sumsq(qlat, PS(3))
            rstd_from_ps(PS(7), 512, rstd[1])
            rstd_from_ps(PS(3), 768, rstd[0])
            for c in range(4):
                stt(ckv[c], kvlat[c], pv("g_kv_lat", c, 1), rstd[1], ALU.mult, ALU.mult)
            for c in range(6):
                stt(qn[c], qlat[c], pv("g_q_lat", c, 1), rstd[0], ALU.mult, ALU.mult)
            ln_pend = []
            for cc in range(8):
                dg_ = dgb[cc % 2]
                if cc >= 2:
                    build_diag(cc)
                pb = PS(cc % 4)
                for j in range(31):
                    mm(pb, dg_.sub(j * 128, j * 128 + 128), hglu[cc].sub(j, j + T), j == 0, j == 30)
                for ra_, rb_, i_ in ln_pend:
                    mm(PS(4), ones_bf, ra_, i_ == 0, i_ == 7)
                    mm(PS(5), ones_bf, rb_, i_ == 0, i_ == 7)
                ln_pend = []
                act(convo[cc], pb, AF.Identity, bias=pv("b_dw", cc, 1))
                ra_ = nring()
                cp(ra_, convo[cc], eng="act")
                rb_ = nring()
                act(rb_, convo[cc], AF.Square)
                ln_pend.append((ra_, rb_, cc))
                cp(halo[cc], hglu[cc].sub(T, T + 30))
                bg_step(2)
            for ra_, rb_, i_ in ln_pend:
                mm(PS(4), ones_bf, ra_, i_ == 0, i_ == 7)
                mm(PS(5), ones_bf, rb_, i_ == 0, i_ == 7)

            LNB = 3
            mu = rstd[0]
            rln = rstd[1]

            def ln_tail():
                ts(mu, PS(4), 1.0 / 1024.0, ALU.mult)
                msq = ntf()
                tt(msq, mu, mu, ALU.mult)
                var = ntf()
                stt(var, PS(5), 1.0 / 1024.0, msq, ALU.mult, ALU.subtract)
                sd = ntf()
                act(sd, var, AF.Sqrt, bias=epsb, scale=1.0)
                recip(rln, sd)

            def ln_c(cc):
                t1 = ntf()
                tt(t1, convo[cc], mu, ALU.subtract)
                t2 = ntf()
                tt(t2, t1, rln, ALU.mult)
                act(convo[cc], t2, AF.Silu, bias=pv("ln_conv_b", cc, 1), scale=pv("ln_conv_g", cc, 1))

            def ln_d(i):
                r = nring()
                act(r, convo[i], AF.Square)
                mm(PS(LNB), ones_bf, r, i == 0, i == 7)
                if i == 7:
                    rstd_from_ps(PS(LNB), 1024, rstd[1])

            def ln_e(cc):
                stt(merged[cc], convo[cc], pv("g_conv_out", cc, 1), rstd[1], ALU.mult, ALU.mult)

            for fn_ in (ln_c, ln_d, ln_e):
                for i in range(8):
                    bg_add(5, lambda fn_=fn_, i=i: fn_(i))

            PB5 = [0, 1, 2, 4, 5]
            pbi = [0]

            def nbank():
                b_ = PB5[pbi[0] % len(PB5)]
                pbi[0] += 1
                return PS(b_)

            for hd in range(NHEAD):
                if hd == 2:
                    ln_tail()
                sl = W.next(wuk[hd], 512)
                pb = nbank()
                for c in range(4):
                    mm(pb, sl.sub(c * 128, c * 128 + 128), ckv[c], c == 0, c == 3)
                cp(knope[hd].sub(t0, t0 + T), pb, eng="act" if (hd % 2 or hd < 4) else "dve")
                if hd >= 2:
                    bg_step(3)
            for half in range(2):
                sl = W.next(wuv[half], 2048)
                for kb in range(4):
                    pb = nbank()
                    for c in range(4):
                        mm(pb, ckv[c].sub(kb * 128, kb * 128 + 128), sl.sub(c * 512, c * 512 + 512), c == 0, c == 3)
                    cp(vc[tt_i * 4 + kb].sub(half * 512, half * 512 + 512), pb, eng="act" if kb % 2 else "dve")
                    bg_step(2)
            qz_all = V(b_h, 0, BF16, 8 * T)
            P.op("dve", lambda e: e.memset(qz_all.ap, 0.0), W=qz_all.k)
            for pr in range(4):
                s_r = W.next(wuqr[pr, 0], 768)
                s_p = W.next(wuqr[pr, 1], 768)
                pr0, pr1 = nbank(), nbank()
                for c in range(6):
                    mm(pr0, s_r.sub(c * 128, c * 128 + 128), qn[c], c == 0, c == 5)
                for c in range(6):
                    mm(pr1, s_p.sub(c * 128, c * 128 + 128), qn[c], c == 0, c == 5)
                t1 = ntf()
                tt(t1, cosf, pr0, ALU.mult)
                t2 = ntf()
                tt(t2, sinf, pr1, ALU.mult)
                tt(qropeZ[2 * pr].part(0, 64), t1.part(0, 64), t2.part(0, 64), ALU.add)
                tt(qropeZ[2 * pr + 1].part(64, 128), t1.part(64, 128), t2.part(64, 128), ALU.add)
                bg_step(2)
            for hd in range(NHEAD):
                sl = W.next(wuqn[hd], 768)
                pb = nbank()
                for c in range(6):
                    mm(pb, sl.sub(c * 128, c * 128 + 128), qn[c], c == 0, c == 5)
                cp(qnope[hd], pb, eng="act" if hd % 2 else "dve")
                bg_step(2)
            bg_flush(5)
            nkb = 4 * tt_i + 4
            units = [(hd, kb) for hd in range(NHEAD) for kb in range(nkb)]
            pts = {}
            LA = 2
            pring = [V(b_uy, 8192 + i * 1024, BF16, T) for i in range(4)]

            def emit_S(i):
                hd, kb = units[i]
                pr, base = hd // 2, 64 * (hd % 2)
                sp_ = PS(i % 3)
                mm(sp_, knope[hd].sub(kb * 128, kb * 128 + 128), qnope[hd], True, False)
                mm(sp_, krope.sub(kb * 128, kb * 128 + 128), qropeZ[hd], False, True)
                pT = pring[i % 4]
                act(pT, sp_, AF.Exp, scale=SCALE)
                if kb >= 4 * tt_i:
                    tt(pT, pT, masks_bf[kb - 4 * tt_i], ALU.mult)
                pts[i] = pT

            def emit_PV(i):
                hd, kb = units[i]
                ob, sb_ = (PS(4), PS(5)) if hd % 2 == 0 else (PS(6), PS(7))
                pT = pts.pop(i)
                mm(ob, vc[kb].sub(hd * 128, hd * 128 + 128), pT, kb == 0, kb == nkb - 1)
                mm(sb_, ones_bf, pT, kb == 0, kb == nkb - 1)
                if kb == nkb - 1:
                    rs_ = ntf()
                    recip(rs_, sb_)
                    tt(attn[hd], rs_, ob, ALU.mult)

            for i in range(len(units) + LA):
                if i < len(units):
                    emit_S(i)
                if i >= LA:
                    emit_PV(i - LA)
            bg_flush(5)
            sumsq(attn, PS(7))
            rstd_from_ps(PS(7), 1024, rstd[2])
            for hd in range(NHEAD):
                stt(merged[8 + hd], attn[hd], pv("g_attn_out", hd, 1), rstd[2], ALU.mult, ALU.mult)
            pend = []
            for dc in range(NDC):
                sl = W.next(wout[dc], 2048)
                pb = PS(dc % 4)
                for kc in range(16):
                    mm(pb, sl.sub(kc * 128, kc * 128 + 128), merged[kc], kc == 0, kc == 15)
                cp(ymix[dc], pb)
                r_ = nring()
                act(r_, ymix[dc], AF.Square)
                pend.append((r_, dc))
                if len(pend) == 3:
                    r0_, i0_ = pend.pop(0)
                    mm(PS(7), ones_bf, r0_, i0_ == 0, i0_ == NDC - 1)
            for r_, idx in pend:
                mm(PS(7), ones_bf, r_, idx == 0, idx == NDC - 1)
            postnorm(1, ymix)

        b_eps = sb("epsb", 4)
        epsb = V(b_eps, 0, F32, 1)

        P.dry = True
        body()
        P.dry = False
        W.reset()
        body()
        keys = P.finalize()
        sems = {}
        for k in keys:
            sems[k] = es.enter_context(nc.semaphore("s_%s_%s" % k))
        with nc.Block() as block:
            @block.sync
            def _(e):
                P.emit("sp", e, sems)

            @block.gpsimd
            def _(e):
                P.emit("pool", e, sems)

            @block.tensor
            def _(e):
                P.emit("pe", e, sems)

            @block.scalar
            def _(e):
                P.emit("act", e, sems)

            @block.vector
            def _(e):
                P.emit("dve", e, sems)
    return nc


def _fm(vec, n):
    return np.ascontiguousarray(np.asarray(vec, np.float32).reshape(n, 128).T)


def _tile_A(w, nk):
    K, N = w.shape
    return np.ascontiguousarray(w.reshape(nk, 128, N // 128, 128).transpose(2, 1, 0, 3).reshape(N // 128, 128, nk * 128))


_NC_CACHE = {}


def kernel(**inp):
    f32 = np.float32
    x = np.asarray(inp["x"], f32)
    B = x.shape[0]
    L = 0
    g = lambda n: np.asarray(inp[n], f32)[L]

    wg1 = _tile_A(g("w1_gate"), 16)
    wu1 = _tile_A(g("w1_up"), 16)
    wg2 = _tile_A(g("w2_gate"), 16)
    wu2 = _tile_A(g("w2_up"), 16)

    def tile_down(w):
        return np.ascontiguousarray(w.reshape(11, 4, 128, 4, 512).transpose(3, 0, 2, 1, 4).reshape(4, 11, 128, 2048))
    wd1 = tile_down(g("w1_down"))
    wd2 = tile_down(g("w2_down"))
    w_in = g("w_in")
    win = _tile_A(w_in[:, :3328], 16)
    rr = w_in[:, 3328:3392]
    rp = np.concatenate([rr[:, 32:], rr[:, :32]], axis=1)
    winr = np.stack([_tile_A(np.concatenate([rr, rr], 1), 16)[0], _tile_A(np.concatenate([rp, rp], 1), 16)[0]])
    w_uq = g("w_uq").reshape(768, 8, 192)
    wuqn = np.stack([_tile_A(np.ascontiguousarray(w_uq[:, h, :128]), 6)[0] for h in range(8)])
    wuqr = np.zeros((4, 2, 128, 768), f32)
    for pr in range(4):
        ra, rb = w_uq[:, 2 * pr, 128:], w_uq[:, 2 * pr + 1, 128:]
        raw = np.concatenate([ra, rb], 1)
        prm = np.concatenate([ra[:, 32:], ra[:, :32], rb[:, 32:], rb[:, :32]], 1)
        wuqr[pr, 0] = _tile_A(np.ascontiguousarray(raw), 6)[0]
        wuqr[pr, 1] = _tile_A(np.ascontiguousarray(prm), 6)[0]
    wuk = _tile_A(g("w_uk"), 4)
    w_uv = g("w_uv")
    wuv = np.ascontiguousarray(w_uv.reshape(4, 128, 2, 512).transpose(2, 1, 0, 3).reshape(2, 128, 2048))
    wout = _tile_A(g("w_out"), 16)
    wada = _tile_A(g("w_ada"), 16)

    consts = np.zeros((128, NCONST), f32)
    consts[:, C_ID:C_ID + 128] = np.eye(128, dtype=f32)
    kk = np.arange(128)[:, None]
    qq = np.arange(512)[None, :]
    xx = np.arange(896)[None, :]
    consts[:, C_MASK:C_MASK + 896] = ((xx - 384) >= kk).astype(f32)
    inv_freq = (10000.0 ** (-np.arange(32, dtype=f32) / f32(32))).astype(f32)
    consts[:, C_INVF] = inv_freq[np.arange(128) % 32]
    consts[:, C_SIGN] = np.where((np.arange(128) % 64) < 32, -1.0, 1.0)

    pv_common = np.zeros((128, NPV), f32)
    def put(name, arr):
        o, w = PV[name]
        pv_common[:, o:o + w] = arr
    for n in ["g_pre_ffn1", "g_post_ffn1", "g_pre_mix", "g_post_mix", "g_pre_ffn2", "g_post_ffn2"]:
        put(n, _fm(g(n), 16))
    put("b_ada", _fm(g("b_ada"), 144))
    for n in ["b_dw", "ln_conv_g", "ln_conv_b", "g_conv_out", "g_attn_out"]:
        put(n, _fm(g(n), 8))
    put("g_q_lat", _fm(g("g_q_lat"), 6))
    put("g_kv_lat", _fm(g("g_kv_lat"), 4))
    wdw = g("w_dw")
    put("w_dw", np.ascontiguousarray(wdw.reshape(31, 8, 128).transpose(2, 1, 0).reshape(128, 248)))

    c = np.asarray(inp["c"], f32)
    pos = np.asarray(inp["positions"], np.int32)
    in_maps = []
    for b in range(B):
        pvb = pv_common.copy()
        o, w = PV["c"]
        pvb[:, o:o + w] = _fm(c[b], 16)
        in_maps.append(dict(
            xT=np.ascontiguousarray(x[b].T), posb=np.ascontiguousarray(np.broadcast_to(pos[b][None, :], (128, S))),
            pvec=pvb, consts=consts, wg1=wg1, wu1=wu1, wd1=wd1, wg2=wg2, wu2=wu2, wd2=wd2, win=win, winr=winr,
            wuqn=wuqn, wuqr=wuqr, wuk=wuk, wuv=wuv, wout=wout, wada=wada))
    if "nc" not in _NC_CACHE:
        _NC_CACHE["nc"] = build_program()
    nc = _NC_CACHE["nc"]
    res = run_bass_kernel_spmd(nc, in_maps, core_ids=list(range(B)))
    out = np.stack([np.ascontiguousarray(res.results[b]["outT"].T) for b in range(B)]).astype(f32)
    return out
```
